# Optimizing a Trainium2 kernel written in Bass

```python
import math
import jax, jax.numpy as jnp
from jax import lax
import numpy as np

D_MODEL = 2048
BATCH = 16
SEQ = 2048
DEPTH = 2

DIFF_HEADS = D_MODEL // 256
DIFF_HALF_DIM = 64
DIFF_VDIM = 2 * DIFF_HALF_DIM
DIFF_QK_WIDTH = DIFF_HEADS * 2 * DIFF_HALF_DIM
DIFF_WIDTH = DIFF_HEADS * DIFF_VDIM
DIL_HEADS = D_MODEL // 256
DIL_HEAD_DIM = 128
DIL_WIDTH = DIL_HEADS * DIL_HEAD_DIM
DIL_CONFIGS = ((128, 1), (512, 4), (2048, 16))
BLOCK = 128
MIX_WIDTH = DIFF_WIDTH + DIL_WIDTH
IN_COLS = 2 * DIFF_QK_WIDTH + DIFF_WIDTH + 3 * DIL_WIDTH
MEM_LEN = 256
MEM_HEADS = 4
MEM_HEAD_DIM = D_MODEL // MEM_HEADS
D_FF = 256 * (-(-(8 * D_MODEL // 3) // 256))
CONV_WIDTH = 3
ALPHA = (2 * DEPTH) ** 0.25
BETA = (8 * DEPTH) ** -0.25
LN_EPS = 1e-5
RMS_EPS = 1e-5

kernel_name = "hybrid_diffattn_dilated_deepnorm_block"


def alibi_slopes(n):
    return (2.0 ** (-8.0 * np.arange(1, n + 1) / n)).astype(np.float32)


def layer_norm(x, g, b):
    xf = x.astype(jnp.float32)
    mu = jnp.mean(xf, -1, keepdims=True)
    var = jnp.mean(jnp.square(xf - mu), -1, keepdims=True)
    return ((xf - mu) * lax.rsqrt(var + LN_EPS) * g.astype(jnp.float32) + b.astype(jnp.float32)).astype(x.dtype)


def rms_norm(x, g):
    xf = x.astype(jnp.float32)
    y = xf * lax.rsqrt(jnp.mean(jnp.square(xf), -1, keepdims=True) + RMS_EPS)
    return (y * g.astype(jnp.float32)).astype(x.dtype)


def diff_attention(q, k, v, lam, slopes):
    S = q.shape[1]
    dk = q.shape[-1]
    scale = dk ** -0.5
    outs = []
    for i in range(S // BLOCK):
        q0, kend = i * BLOCK, (i + 1) * BLOCK
        s = jnp.einsum('bqhcd,bkhcd->bhcqk', q[:, q0:kend], k[:, :kend]).astype(jnp.float32) * scale
        dist = (q0 + jnp.arange(BLOCK))[:, None] - jnp.arange(kend)[None, :]
        bias = -slopes[:, None, None, None] * dist.astype(jnp.float32)
        s = jnp.where(dist >= 0, s + bias, -jnp.inf)
        p = jax.nn.softmax(s, axis=-1)
        a = p[:, :, 0] - lam * p[:, :, 1]
        outs.append(jnp.einsum('bhqk,bkhd->bqhd', a.astype(v.dtype), v[:, :kend]))
    return jnp.concatenate(outs, axis=1)


def dilated_branch(q, k, v, slopes, window, dil):
    B, S, H, dh = q.shape
    L = S // dil
    nb = -(-L // BLOCK)
    Lp = nb * BLOCK
    n_back = window // dil
    scale = dh ** -0.5

    def to_sub(t):
        t = t.reshape(B, L, dil, H, dh).transpose(0, 2, 1, 3, 4)
        t = jnp.pad(t, ((0, 0), (0, 0), (0, Lp - L), (0, 0), (0, 0)))
        return t.reshape(B, dil, nb, BLOCK, H, dh)

    def with_prev(t):
        prev = jnp.pad(t, ((0, 0), (0, 0), (1, 0), (0, 0), (0, 0), (0, 0)))[:, :, :-1]
        return jnp.concatenate([prev, t], axis=3)

    qs = to_sub(q)
    kb, vb = with_prev(to_sub(k)), with_prev(to_sub(v))
    s = jnp.einsum('bcnqhd,bcnkhd->bchnqk', qs, kb).astype(jnp.float32) * scale
    qi = jnp.arange(BLOCK)[:, None]
    kj = jnp.arange(2 * BLOCK)[None, :]
    rel = qi + BLOCK - kj
    key_sub = jnp.arange(nb)[:, None, None] * BLOCK - BLOCK + kj[None]
    valid = (rel >= 0) & (rel <= n_back) & (key_sub >= 0)
    bias = -slopes[:, None, None, None] * (rel * dil).astype(jnp.float32)
    s = jnp.where(valid, s + bias, -jnp.inf)
    m = jnp.max(s, -1, keepdims=True)
    e = jnp.exp(s - m)
    l = jnp.sum(e, -1, keepdims=True)
    p = e / l
    lse = (m + jnp.log(l))[..., 0]
    o = jnp.einsum('bchnqk,bcnkhd->bcnqhd', p.astype(v.dtype), vb)
    o = o.reshape(B, dil, Lp, H, dh)[:, :, :L].transpose(0, 2, 1, 3, 4).reshape(B, S, H, dh)
    lse = lse.transpose(0, 1, 3, 4, 2).reshape(B, dil, Lp, H)[:, :, :L].transpose(0, 2, 1, 3).reshape(B, S, H)
    return o, lse


def dilated_attention(q, k, v, slopes):
    outs, lses = [], []
    for window, dil in DIL_CONFIGS:
        o, lse = dilated_branch(q, k, v, slopes, window, dil)
        outs.append(o)
        lses.append(lse)
    w = jax.nn.softmax(jnp.stack(lses, 0), axis=0)
    return jnp.einsum('cbsh,cbshd->bshd', w.astype(q.dtype), jnp.stack(outs, 0))


def hybrid_mixer(x, w_in, w_out, lq1, lk1, lq2, lk2, g_diff, g_dil, layer_idx):
    B, S, _ = x.shape
    proj = x @ w_in
    c1 = DIFF_QK_WIDTH
    c2 = 2 * DIFF_QK_WIDTH
    c3 = c2 + DIFF_WIDTH
    c4 = c3 + DIL_WIDTH
    c5 = c4 + DIL_WIDTH
    dq = proj[..., :c1].reshape(B, S, DIFF_HEADS, 2, DIFF_HALF_DIM)
    dk = proj[..., c1:c2].reshape(B, S, DIFF_HEADS, 2, DIFF_HALF_DIM)
    dv = proj[..., c2:c3].reshape(B, S, DIFF_HEADS, DIFF_VDIM)
    sq = proj[..., c3:c4].reshape(B, S, DIL_HEADS, DIL_HEAD_DIM)
    sk = proj[..., c4:c5].reshape(B, S, DIL_HEADS, DIL_HEAD_DIM)
    sv = proj[..., c5:].reshape(B, S, DIL_HEADS, DIL_HEAD_DIM)

    slopes = jnp.asarray(alibi_slopes(DIFF_HEADS + DIL_HEADS))
    slopes_diff, slopes_dil = slopes[0::2], slopes[1::2]

    lam_init = 0.8 - 0.6 * math.exp(-0.3 * layer_idx)
    f32 = jnp.float32
    lam = (jnp.exp(jnp.sum(lq1.astype(f32) * lk1.astype(f32)))
           - jnp.exp(jnp.sum(lq2.astype(f32) * lk2.astype(f32))) + lam_init)
    o_diff = diff_attention(dq, dk, dv, lam, slopes_diff)
    o_diff = rms_norm(o_diff, g_diff) * (1.0 - lam_init)
    o_dil = rms_norm(dilated_attention(sq, sk, sv, slopes_dil), g_dil)
    o = jnp.concatenate([o_diff.reshape(B, S, DIFF_WIDTH), o_dil.reshape(B, S, DIL_WIDTH)], axis=-1)
    return o @ w_out


def memory_attention(x, mem, w_q, w_kv, w_o):
    B, S, _ = x.shape
    M = mem.shape[1]
    q = (x @ w_q).reshape(B, S, MEM_HEADS, MEM_HEAD_DIM)
    kv = (mem @ w_kv).reshape(B, M, 2, MEM_HEADS, MEM_HEAD_DIM)
    s = jnp.einsum('bshd,bmhd->bhsm', q, kv[:, :, 0]).astype(jnp.float32) * MEM_HEAD_DIM ** -0.5
    p = jax.nn.softmax(s, axis=-1)
    o = jnp.einsum('bhsm,bmhd->bshd', p.astype(x.dtype), kv[:, :, 1]).reshape(B, S, D_MODEL)
    return o @ w_o


def conv_ffn(x, w_up, conv_w, conv_b, w_down):
    S = x.shape[1]
    h = x @ w_up
    hp = jnp.pad(h, ((0, 0), (CONV_WIDTH - 1, 0), (0, 0)))
    h = hp[:, 0:S] * conv_w[0] + hp[:, 1:S + 1] * conv_w[1] + hp[:, 2:S + 2] * conv_w[2] + conv_b
    gate, up = h[..., :D_FF], h[..., D_FF:]
    return (jax.nn.silu(gate) * up) @ w_down


def setup_inputs(seed: int = 0) -> dict:
    key = jax.random.key(seed)
    ks = jax.random.split(key, 32)
    f32 = jnp.float32

    def nrm(k, shape, std):
        return jax.random.normal(k, shape, f32) * std

    d_std = D_MODEL ** -0.5
    x = nrm(ks[0], (BATCH, SEQ, D_MODEL), 1.0)
    mem = nrm(ks[1], (BATCH, MEM_LEN, D_MODEL), 1.0)
    w_in = jnp.concatenate([
        nrm(ks[2], (DEPTH, D_MODEL, 2 * DIFF_QK_WIDTH), d_std),
        nrm(ks[3], (DEPTH, D_MODEL, DIFF_WIDTH), d_std * BETA),
        nrm(ks[4], (DEPTH, D_MODEL, 2 * DIL_WIDTH), d_std),
        nrm(ks[5], (DEPTH, D_MODEL, DIL_WIDTH), d_std * BETA),
    ], axis=-1)
    w_mix_out = nrm(ks[6], (DEPTH, MIX_WIDTH, D_MODEL), MIX_WIDTH ** -0.5 * BETA)
    lambda_q1 = nrm(ks[7], (DEPTH, DIFF_HALF_DIM), 0.1)
    lambda_k1 = nrm(ks[8], (DEPTH, DIFF_HALF_DIM), 0.1)
    lambda_q2 = nrm(ks[9], (DEPTH, DIFF_HALF_DIM), 0.1)
    lambda_k2 = nrm(ks[10], (DEPTH, DIFF_HALF_DIM), 0.1)
    g_diff = 1.0 + nrm(ks[11], (DEPTH, DIFF_VDIM), 0.02)
    g_dil = 1.0 + nrm(ks[12], (DEPTH, DIL_HEAD_DIM), 0.02)
    ln1_g = 1.0 + nrm(ks[13], (DEPTH, D_MODEL), 0.02)
    ln1_b = nrm(ks[14], (DEPTH, D_MODEL), 0.02)
    w_mem_q = nrm(ks[15], (DEPTH, D_MODEL, D_MODEL), d_std)
    w_mem_kv = jnp.concatenate([
        nrm(ks[16], (DEPTH, D_MODEL, D_MODEL), d_std),
        nrm(ks[17], (DEPTH, D_MODEL, D_MODEL), d_std * BETA),
    ], axis=-1)
    w_mem_o = nrm(ks[18], (DEPTH, D_MODEL, D_MODEL), d_std * BETA)
    ln2_g = 1.0 + nrm(ks[19], (DEPTH, D_MODEL), 0.02)
    ln2_b = nrm(ks[20], (DEPTH, D_MODEL), 0.02)
    w_up = nrm(ks[21], (DEPTH, D_MODEL, 2 * D_FF), d_std * BETA)
    conv_w = nrm(ks[22], (DEPTH, CONV_WIDTH, 2 * D_FF), CONV_WIDTH ** -0.5)
    conv_b = nrm(ks[23], (DEPTH, 2 * D_FF), 0.02)
    w_down = nrm(ks[24], (DEPTH, D_FF, D_MODEL), D_FF ** -0.5 * BETA)
    ln3_g = 1.0 + nrm(ks[25], (DEPTH, D_MODEL), 0.02)
    ln3_b = nrm(ks[26], (DEPTH, D_MODEL), 0.02)
    return {"x": x, "mem": mem, "w_in": w_in, "w_mix_out": w_mix_out,
            "lambda_q1": lambda_q1, "lambda_k1": lambda_k1, "lambda_q2": lambda_q2, "lambda_k2": lambda_k2,
            "g_diff": g_diff, "g_dil": g_dil, "ln1_g": ln1_g, "ln1_b": ln1_b,
            "w_mem_q": w_mem_q, "w_mem_kv": w_mem_kv, "w_mem_o": w_mem_o, "ln2_g": ln2_g, "ln2_b": ln2_b,
            "w_up": w_up, "conv_w": conv_w, "conv_b": conv_b, "w_down": w_down, "ln3_g": ln3_g, "ln3_b": ln3_b}


def reference(x, mem, w_in, w_mix_out, lambda_q1, lambda_k1, lambda_q2, lambda_k2, g_diff, g_dil,
              ln1_g, ln1_b, w_mem_q, w_mem_kv, w_mem_o, ln2_g, ln2_b,
              w_up, conv_w, conv_b, w_down, ln3_g, ln3_b):
    h = x
    for l in range(DEPTH):
        h = layer_norm(ALPHA * h + hybrid_mixer(h, w_in[l], w_mix_out[l], lambda_q1[l], lambda_k1[l],
                                                lambda_q2[l], lambda_k2[l], g_diff[l], g_dil[l], l),
                       ln1_g[l], ln1_b[l])
        h = layer_norm(ALPHA * h + memory_attention(h, mem, w_mem_q[l], w_mem_kv[l], w_mem_o[l]),
                       ln2_g[l], ln2_b[l])
        h = layer_norm(ALPHA * h + conv_ffn(h, w_up[l], conv_w[l], conv_b[l], w_down[l]),
                       ln3_g[l], ln3_b[l])
    return h
```

```python
import math
import numpy as np
import concourse.bass as bass
import concourse.mybir as mybir
from concourse.bass_utils import run_bass_kernel_spmd

F32 = mybir.dt.float32
BF16 = mybir.dt.bfloat16
AF = mybir.ActivationFunctionType
ALU = mybir.AluOpType
AX = mybir.AxisListType

D = 2048
S = 2048
NB = 16
NCORES = 8
SPC = NB // NCORES
DEPTH = 2
MEM = 256
DFF = 5632
NFC = DFF // 128
ALPHA = (2 * DEPTH) ** 0.25
INV_ALPHA = 1.0 / ALPHA
LN_EPS = 1e-5
RMS_EPS = 1e-5
BIG = 1.0e7
TT = 512
NTT = S // TT

_sl = (2.0 ** (-8.0 * np.arange(1, 17) / 16)).astype(np.float32)
SLOPES_DIFF = [float(v) for v in _sl[0::2]]
SLOPES_DIL = [float(v) for v in _sl[1::2]]

PP_LN = 0
PP_CW = 96
PP_CB = PP_CW + 3 * 88
PP_GD = PP_CB + 88
PP_GL = PP_GD + 1
PP_LAM = PP_GL + 1
NP = PP_LAM + 256
C_ID = 0
C_DG = 128
C_DD = C_DG + 512
C_CM = C_DD + 512
C_ON = C_CM + 1152
NCF = C_ON + 128

_ESZ = {F32: 4, BF16: 2}
GRAN = 256


class _Op:
    __slots__ = ("eng", "fn", "deps", "dma", "sig", "signo", "dcount")

    def __init__(self, eng, fn, dma):
        self.eng = eng
        self.fn = fn
        self.deps = ()
        self.dma = dma
        self.sig = False
        self.signo = 0
        self.dcount = 0


class Prog:
    CAP = 30000

    def __init__(self):
        self.ops = []
        self.lastw = {}
        self.readers = {}
        self.nsync = True
        self.dry = False

    def _grans(self, ap):
        t = ap.tensor
        esz = _ESZ[ap.dtype]
        pat = ap.ap
        pstride = pat[0][0]
        off = ap.offset % pstride if pstride > 0 else ap.offset
        ext = 1
        for st, cnt in pat[1:]:
            ext += (cnt - 1) * abs(st)
        b0 = off * esz
        b1 = (off + ext) * esz
        name = t.name
        return [(name, g) for g in range(b0 // GRAN, (b1 - 1) // GRAN + 1)]

    def add(self, eng, fn, reads=(), writes=(), dma=None):
        if self.dry:
            return -1
        op = _Op(eng, fn, dma)
        idx = len(self.ops)
        ops = self.ops
        lastw = self.lastw
        readers = self.readers
        mykey = ("d", dma) if dma is not None else ("e", eng)
        best = {}

        def dep(d):
            o = ops[d]
            k = ("d", o.dma) if o.dma is not None else ("e", o.eng)
            if best.get(k, -1) < d:
                best[k] = d

        rg = []
        for ap in reads:
            rg.extend(self._grans(ap))
        wg = []
        for ap in writes:
            wg.extend(self._grans(ap))
        for g in rg:
            w = lastw.get(g)
            if w is not None:
                dep(w)
        for g in wg:
            w = lastw.get(g)
            if w is not None:
                dep(w)
            r = readers.get(g)
            if r:
                for d in r.values():
                    dep(d)
        for g in rg:
            r = readers.get(g)
            if r is None:
                readers[g] = {mykey: idx}
            else:
                r[mykey] = idx
        for g in wg:
            lastw[g] = idx
            readers[g] = None
        fdeps = []
        for k, d in best.items():
            o = ops[d]
            if o.dma is None and o.eng == eng and dma is None:
                if eng == "pe" or not self.nsync:
                    continue
            o.sig = True
            fdeps.append(d)
        op.deps = fdeps
        ops.append(op)
        return idx

    def emit(self, nc, block):
        ops = self.ops
        ecount = {}
        dcount = {}
        for op in ops:
            if op.dma is not None:
                dcount[op.dma] = dcount.get(op.dma, 0) + 1
                op.dcount = dcount[op.dma]
            elif op.sig:
                ecount[op.eng] = ecount.get(op.eng, 0) + 1
                op.signo = ecount[op.eng]
        CAP = self.CAP
        DCAP = CAP // 16
        sems = {}

        def sem_for(key):
            s = sems.get(key)
            if s is None:
                s = nc.alloc_semaphore(name="s_%s_%d" % (str(key[0]), key[1]))
                sems[key] = s
            return s

        def sem_of(op):
            if op.dma is not None:
                n = op.dcount - 1
                return ("d_" + op.dma, n // DCAP), (n % DCAP + 1) * 16
            n = op.signo - 1
            return (op.eng, n // CAP), n % CAP + 1

        for op in ops:
            if op.dma is not None or op.sig:
                sem_for(sem_of(op)[0])
        per_eng = {"pe": [], "act": [], "dve": [], "pool": [], "sp": []}
        for op in ops:
            per_eng[op.eng].append(op)

        def run(e, lst):
            waited = {}
            for op in lst:
                need = {}
                for d in op.deps:
                    k, v = sem_of(ops[d])
                    if need.get(k, 0) < v:
                        need[k] = v
                for k, v in need.items():
                    if waited.get(k, 0) >= v:
                        continue
                    e.wait_ge(sems[k], v)
                    waited[k] = v
                ins = op.fn(e)
                if op.dma is not None:
                    ins.then_inc(sems[sem_of(op)[0]], 16)
                elif op.sig:
                    ins.then_inc(sems[sem_of(op)[0]], 1)

        @block.tensor
        def _(e):
            run(e, per_eng["pe"])

        @block.scalar
        def _(e):
            run(e, per_eng["act"])

        @block.vector
        def _(e):
            run(e, per_eng["dve"])

        @block.gpsimd
        def _(e):
            run(e, per_eng["pool"])

        @block.sync
        def _(e):
            run(e, per_eng["sp"])

        return sems


def build_program(n_seq=SPC, n_layers=DEPTH, stop_after=None, nsync=True):
    nc = bass.Bass("TRN2", target_bir_lowering=False)
    L = DEPTH
    x_d = nc.dram_tensor("x", [SPC, S, D], F32, kind="ExternalInput").ap()
    mem_d = nc.dram_tensor("mem", [SPC, MEM, D], F32, kind="ExternalInput").ap()
    win_d = nc.dram_tensor("w_in", [L, 24, 128, 16, 256], F32, kind="ExternalInput").ap()
    wout_d = nc.dram_tensor("w_out", [L, 8, 128, 16, 256], F32, kind="ExternalInput").ap()
    wq_d = nc.dram_tensor("w_q", [L, 8, 128, 16, 256], F32, kind="ExternalInput").ap()
    wkv_d = nc.dram_tensor("w_kv", [L, 16, 128, 16, 256], F32, kind="ExternalInput").ap()
    wo_d = nc.dram_tensor("w_o", [L, 8, 128, 16, 256], F32, kind="ExternalInput").ap()
    wup_d = nc.dram_tensor("w_up", [L, 44, 128, 2, 16, 128], F32, kind="ExternalInput").ap()
    wdn_d = nc.dram_tensor("w_dn", [L, 16, 2, 128, 22, 128], F32, kind="ExternalInput").ap()
    pp_d = nc.dram_tensor("pp", [128, L, NP], F32, kind="ExternalInput").ap()
    cst_d = nc.dram_tensor("cst", [128, NCF], F32, kind="ExternalInput").ap()
    out_d = nc.dram_tensor("out", [SPC, S, D], F32, kind="ExternalOutput").ap()

    P = Prog()
    P.nsync = nsync
    wreqs = []

    import contextlib
    stack = contextlib.ExitStack()
    with stack:
        def SB(name, shape, dt):
            return stack.enter_context(nc.sbuf_tensor("sb_" + name, shape, dt))

        hT = SB("hT", [128, 16, S], BF16)
        aT = SB("aT", [128, 32768], BF16)
        wb = [SB("wb%d" % i, [128, 4096], BF16) for i in range(2)]
        scr = SB("scr", [128, 8192], F32)
        tmp = SB("tmp", [128, 6, 516], F32)
        pTb = SB("pT", [128, 2, 512], BF16)
        cf = SB("cf", [128, C_CM], F32)
        cmask = SB("cmask", [128, 1152], BF16)
        ones_b = SB("ones_b", [128, 128], BF16)
        pp = SB("pp", [128, L, NP], F32)
        halo = SB("halo", [128, 88, 2], F32)
        sm = SB("sm", [128, 64], F32)
        ps = stack.enter_context(nc.psum_tensor("ps", [128, 8, 512], F32))

        ident = cf[:, C_ID:C_ID + 128]
        distg = cf[:, C_DG:C_DG + 512]
        distd = cf[:, C_DD:C_DD + 512]

        oT = aT[:, :].rearrange("p (k t) -> p k t", k=16)
        scr_b = scr[:, :].bitcast(BF16)
        qT2 = scr_b[:, 0:4096].rearrange("p (h t) -> p h t", h=2)
        kT2 = scr_b[:, 4096:8192].rearrange("p (h t) -> p h t", h=2)
        vA = scr_b[:, 8192:8192 + 16 * 2 * 130].rearrange("p (b h c) -> p b h c", b=16, h=2)
        zT = scr[:, :].rearrange("p (k t) -> p k t", k=16)
        xst = scr[:, :].rearrange("p (i c) -> p i c", i=4)

        qTm = aT[:, 0:8192].rearrange("p (k t) -> p k t", k=16)
        omT = aT[:, 8192:16384].rearrange("p (k t) -> p k t", k=16)
        memT = aT[:, 16384:20480].rearrange("p (k t) -> p k t", k=16)
        KmT = aT[:, 20480:24576].rearrange("p (k t) -> p k t", k=16)
        Vm = aT[:, 24576:28672].rearrange("p (b c) -> p b c", b=2)
        gT = aT[:, 0:NFC * 512].rearrange("p (k t) -> p k t", k=NFC)
        ostage = aT[:, NFC * 512:NFC * 512 + 8192].bitcast(F32).rearrange("p (i c) -> p i c", i=2)

        state = {"w": 0, "bank": 0, "sbank": 0, "abank": 0, "ev": 0, "x": 0, "os": 0}

        def _issue_w(k):
            src_ap, nk, ncols = wreqs[k]
            i = k % 2
            dst = wb[i][:, 0:nk * ncols].rearrange("p (k c) -> p k c", k=nk)
            P.add("pool", lambda e, d=dst, s=src_ap: e.dma_start(out=d, in_=s), writes=[dst], dma="w%d" % i)

        def load_w(src_ap, nk, ncols):
            k = state["w"]
            state["w"] += 1
            if P.dry:
                wreqs.append((src_ap, nk, ncols))
            else:
                if k == 0:
                    _issue_w(0)
                if k + 1 < len(wreqs):
                    _issue_w(k + 1)
            i = k % 2
            return wb[i][:, 0:nk * ncols].rearrange("p (k c) -> p k c", k=nk)

        def gbank():
            b = state["bank"] % 2
            state["bank"] += 1
            return b

        def mm(out, lhsT, rhs, start, stop, skip=False):
            P.add("pe", lambda e, o=out, l=lhsT, r=rhs, s0=start, s1=stop, sk=skip: e.matmul(o, lhsT=l, rhs=r, start=s0, stop=s1, skip_group_check=sk),
                  reads=[lhsT, rhs], writes=[out])

        def tr(out, in_):
            P.add("pe", lambda e, o=out, i=in_: e.transpose(o, i, ident), reads=[in_, ident], writes=[out])

        def act(out, in_, func, bias=0.0, scale=1.0, extra_reads=()):
            rd = [in_] + list(extra_reads)
            if not isinstance(bias, float):
                rd.append(bias)
            if not isinstance(scale, float):
                rd.append(scale)
            P.add("act", lambda e, o=out, i=in_, f=func, b=bias, s=scale: e.activation(out=o, in_=i, func=f, bias=b, scale=s),
                  reads=rd, writes=[out])

        def ts(eng, out, in0, s1, s2, op0, op1=None):
            rd = [in0]
            if not isinstance(s1, float):
                rd.append(s1)
            if s2 is not None and not isinstance(s2, float):
                rd.append(s2)
            if op1 is None:
                P.add(eng, lambda e, o=out, i=in0, a=s1, p0=op0: e.tensor_scalar(out=o, in0=i, scalar1=a, scalar2=None, op0=p0),
                      reads=rd, writes=[out])
            else:
                P.add(eng, lambda e, o=out, i=in0, a=s1, b=s2, p0=op0, p1=op1: e.tensor_scalar(out=o, in0=i, scalar1=a, scalar2=b, op0=p0, op1=p1),
                      reads=rd, writes=[out])

        def stt(eng, out, in0, scalar, in1, op0, op1):
            rd = [in0, in1]
            if not isinstance(scalar, float):
                rd.append(scalar)
            P.add(eng, lambda e, o=out, i0=in0, s=scalar, i1=in1, p0=op0, p1=op1: e.scalar_tensor_tensor(out=o, in0=i0, scalar=s, in1=i1, op0=p0, op1=p1),
                  reads=rd, writes=[out])

        def tt(eng, out, in0, in1, op):
            P.add(eng, lambda e, o=out, i0=in0, i1=in1, p=op: e.tensor_tensor(out=o, in0=i0, in1=i1, op=p),
                  reads=[in0, in1], writes=[out])

        def cp(eng, out, in_):
            if eng == "act":
                P.add("act", lambda e, o=out, i=in_: e.copy(out=o, in_=i), reads=[in_], writes=[out])
            else:
                P.add(eng, lambda e, o=out, i=in_: e.tensor_copy(out=o, in_=i), reads=[in_], writes=[out])

        def evac(out, in_, scale=None):
            k = state["ev"]
            state["ev"] += 1
            if k % 2 == 0:
                if scale is None:
                    cp("act", out, in_)
                else:
                    P.add("act", lambda e, o=out, i=in_, s=scale: e.mul(out=o, in_=i, mul=s), reads=[in_], writes=[out])
            else:
                if scale is None:
                    cp("dve", out, in_)
                else:
                    ts("dve", out, in_, scale, None, ALU.mult)

        def memset(eng, ap, val):
            P.add(eng, lambda e, a=ap, v=val: e.memset(a, v), writes=[ap])

        def recip(out, in_):
            P.add("dve", lambda e, o=out, i=in_: e.reciprocal(out=o, in_=i), reads=[in_], writes=[out])

        P.add("sp", lambda e: e.dma_start(out=cf[:, :], in_=cst_d[:, 0:C_CM]), writes=[cf[:, :]], dma="c0")
        P.add("pool", lambda e: e.dma_start(out=cmask[:, :], in_=cst_d[:, C_CM:C_CM + 1152]), writes=[cmask[:, :]], dma="c1")
        P.add("pool", lambda e: e.dma_start(out=ones_b[:, :], in_=cst_d[:, C_ON:C_ON + 128]), writes=[ones_b[:, :]], dma="c2")
        P.add("sp", lambda e: e.dma_start(out=pp[:, :, :], in_=pp_d[:, :, :]), writes=[pp[:, :, :]], dma="c3")

        for l in range(L):
            lam_init = 0.8 - 0.6 * math.exp(-0.3 * l)
            lq1 = pp[:, l, PP_LAM:PP_LAM + 64]
            lk1 = pp[:, l, PP_LAM + 64:PP_LAM + 128]
            lq2 = pp[:, l, PP_LAM + 128:PP_LAM + 192]
            lk2 = pp[:, l, PP_LAM + 192:PP_LAM + 256]
            t0 = tmp[:, 0, 0:64]
            t1 = tmp[:, 1, 0:64]
            tt("dve", t0, lq1, lk1, ALU.mult)
            P.add("dve", lambda e, o=sm[:, 8:9], i=t0: e.reduce_sum(out=o, in_=i, axis=AX.X), reads=[t0], writes=[sm[:, 8:9]])
            tt("dve", t1, lq2, lk2, ALU.mult)
            P.add("dve", lambda e, o=sm[:, 9:10], i=t1: e.reduce_sum(out=o, in_=i, axis=AX.X), reads=[t1], writes=[sm[:, 9:10]])
            act(sm[:, 10:12], sm[:, 8:10], AF.Exp)
            stt("dve", sm[:, l:l + 1], sm[:, 10:11], 1.0, sm[:, 11:12], ALU.mult, ALU.subtract)
            ts("dve", sm[:, l:l + 1], sm[:, l:l + 1], float(lam_init), None, ALU.add)
            ts("dve", sm[:, 2 + l:3 + l], sm[:, l:l + 1], -1.0, None, ALU.mult)
            ts("dve", sm[:, 4 + l:5 + l], pp[:, l, PP_GD:PP_GD + 1], float(1.0 - lam_init), None, ALU.mult)

        def load_x(s):
            for tb in range(16):
                slot = tb % 4
                dst = xst[:, slot, :]
                src = x_d[s, tb * 128:(tb + 1) * 128, :]
                P.add("sp", lambda e, d=dst, s_=src: e.dma_start(out=d, in_=s_), writes=[dst], dma="xs%d" % slot)
                for g in range(4):
                    b = gbank()
                    for j in range(4):
                        kc = g * 4 + j
                        tr(ps[:, b, j * 128:(j + 1) * 128], xst[:, slot, kc * 128:(kc + 1) * 128])
                    evac(hT[:, g * 4:(g + 1) * 4, tb * 128:(tb + 1) * 128],
                         ps[:, b, :].rearrange("p (j c) -> p j c", j=4))

        def attn_qtile(qt, lo, hi, hh, slope, dil, accbase):
            q0 = qt * 512
            nkb = 4 * qt + 4
            accv = ps[:, accbase:accbase + 2, :].rearrange("p b (h c) -> p (b h) c", h=2)
            for kb in range(nkb):
                j = kb - 4 * qt
                if j >= 0:
                    c0 = 128 * j
                    n = 512 - c0
                    dist = distd[:, 0:n]
                    bias = 0.0
                    cm = cmask[:, 0:n]
                else:
                    delta = q0 - kb * 128
                    c0 = 0
                    n = 512
                    dist = distg
                    bias = float(-slope * delta)
                    cmo = min(delta, 640)
                    cm = cmask[:, cmo:cmo + 512]
                sbk = 2 + (state["sbank"] % 2)
                i = state["sbank"] % 2
                state["sbank"] += 1
                pss = ps[:, sbk, 0:n]
                mm(pss, kT2[lo:hi, hh, kb * 128:(kb + 1) * 128], qT2[lo:hi, hh, q0 + c0:q0 + 512], True, True)
                xs_ = tmp[:, i, 0:n]
                stt("dve", xs_, dist, float(-slope), pss, ALU.mult, ALU.add)
                pt = pTb[:, i, 0:n]
                act(pt, xs_, AF.Exp, bias=bias)
                if dil:
                    tt("pool", pt, pt, cm, ALU.mult)
                for s_ in range(max(j, 0), 4):
                    col = s_ * 128 - c0
                    mm(accv[:, s_, 0:129], pTb[:, i, col:col + 128], vA[:, kb, hh, 0:129],
                       kb == 0 and s_ % 2 == 0, kb == 4 * qt + s_, skip=True)
            return accv

        def attn_finish(accv, qt, chunk, gcol, first_map=False, second_map=False, lamneg=None):
            q0 = qt * 512
            rec = sm[:, 16:20]
            osb = tmp[:, 2, 0:512].rearrange("p (s c) -> p s c", s=4)
            o0 = tmp[:, 3, 0:512].rearrange("p (s c) -> p s c", s=4)
            sq = tmp[:, 4, 0:512].rearrange("p (s c) -> p s c", s=4)
            recip(rec, accv[:, :, 128:129].rearrange("p s c -> p (s c)"))
            if first_map:
                for s_ in range(4):
                    ts("dve", o0[:, s_, :], accv[:, s_, 0:128], rec[:, s_:s_ + 1], None, ALU.mult)
                return
            if second_map:
                ts("dve", rec, rec, lamneg, None, ALU.mult)
                for s_ in range(4):
                    stt("dve", osb[:, s_, :], accv[:, s_, 0:128], rec[:, s_:s_ + 1], o0[:, s_, :], ALU.mult, ALU.add)
            else:
                for s_ in range(4):
                    ts("dve", osb[:, s_, :], accv[:, s_, 0:128], rec[:, s_:s_ + 1], None, ALU.mult)
            tt("pool", sq, osb, osb, ALU.mult)
            ss = sm[:, 20:24]
            P.add("dve", lambda e, o=ss, i=sq: e.reduce_sum(out=o, in_=i, axis=AX.X), reads=[sq], writes=[ss])
            sd = sm[:, 24:28]
            act(sd, ss, AF.Sqrt, bias=sm[:, 32:33], scale=1.0 / 128.0)
            rstd = sm[:, 28:32]
            recip(rstd, sd)
            for s_ in range(4):
                ts("dve", osb[:, s_, :], osb[:, s_, :], rstd[:, s_:s_ + 1], None, ALU.mult)
            b = gbank()
            for s_ in range(4):
                tr(ps[:, b, s_ * 128:(s_ + 1) * 128], osb[:, s_, :])
            act(oT[:, chunk, q0:q0 + 512], ps[:, b, :], AF.Copy, scale=gcol)

        def mixer(l):
            memset("pool", vA[:, :, :, 128:130], 1.0)
            for pair in range(8):
                dil = pair >= 4
                jp = pair - 4 if dil else pair
                base = 12 if dil else 0
                scale = (128 ** -0.5) if dil else (64 ** -0.5)
                wq = load_w(win_d[l, base + jp], 16, 256)
                for hh in range(2):
                    for t_ in range(NTT):
                        b = gbank()
                        for kc in range(16):
                            mm(ps[:, b, :], wq[:, kc, hh * 128:(hh + 1) * 128], hT[:, kc, t_ * TT:(t_ + 1) * TT], kc == 0, kc == 15)
                        evac(qT2[:, hh, t_ * TT:(t_ + 1) * TT], ps[:, b, :], scale=float(scale))
                wk = load_w(win_d[l, base + 4 + jp], 16, 256)
                for hh in range(2):
                    for t_ in range(NTT):
                        b = gbank()
                        for kc in range(16):
                            mm(ps[:, b, :], wk[:, kc, hh * 128:(hh + 1) * 128], hT[:, kc, t_ * TT:(t_ + 1) * TT], kc == 0, kc == 15)
                        evac(kT2[:, hh, t_ * TT:(t_ + 1) * TT], ps[:, b, :])
                wv = load_w(win_d[l, base + 8 + jp], 16, 256)
                for tb in range(16):
                    b = gbank()
                    for kc in range(16):
                        mm(ps[:, b, 0:256], hT[:, kc, tb * 128:(tb + 1) * 128], wv[:, kc, :], kc == 0, kc == 15)
                    evac(vA[:, tb, :, 0:128], ps[:, b, 0:256].rearrange("p (h c) -> p h c", h=2))
                for hh in range(2):
                    head = jp * 2 + hh
                    if dil:
                        slope = SLOPES_DIL[head]
                        chunk = 8 + head
                        for qt in range(NTT):
                            ab = 4 + 2 * (state["abank"] % 2)
                            state["abank"] += 1
                            accv = attn_qtile(qt, 0, 128, hh, slope, True, ab)
                            attn_finish(accv, qt, chunk, pp[:, l, PP_GL:PP_GL + 1])
                    else:
                        slope = SLOPES_DIFF[head]
                        chunk = head
                        for qt in range(NTT):
                            ab = 4 + 2 * (state["abank"] % 2)
                            state["abank"] += 1
                            accv = attn_qtile(qt, 0, 64, hh, slope, False, ab)
                            attn_finish(accv, qt, chunk, None, first_map=True)
                            ab = 4 + 2 * (state["abank"] % 2)
                            state["abank"] += 1
                            accv = attn_qtile(qt, 64, 128, hh, slope, False, ab)
                            attn_finish(accv, qt, chunk, sm[:, 4 + l:5 + l], second_map=True, lamneg=sm[:, 2 + l:3 + l])

        def second_gemm_ln(l, t_, rhs_fn, nk, slab_fn, lncol, final=None):
            tsl = slice(t_ * TT, (t_ + 1) * TT)
            for cb in range(16):
                b = gbank()
                k = 0
                for (wv_, kcs, coff) in slab_fn(cb):
                    for (wi, kc) in kcs:
                        mm(ps[:, b, :], wv_[:, wi, coff:coff + 128], rhs_fn(kc), k == 0, k == nk - 1)
                        k += 1
                assert k == nk
                stt("dve", zT[:, cb, :], ps[:, b, :], float(INV_ALPHA), hT[:, cb, tsl], ALU.mult, ALU.add)
                zb = pTb[:, 0, :]
                zq = pTb[:, 1, :]
                cp("act", zb, zT[:, cb, :])
                act(zq, zT[:, cb, :], AF.Square)
                mm(ps[:, 4, :], ones_b[:, :], zb, cb == 0, cb == 15)
                mm(ps[:, 5, :], ones_b[:, :], zq, cb == 0, cb == 15)
            mean = tmp[:, 0, 0:512]
            rstd = tmp[:, 1, 0:512]
            nmr = tmp[:, 5, 0:512]
            ts("dve", mean, ps[:, 4, :], 1.0 / D, None, ALU.mult)
            tt("dve", nmr, mean, mean, ALU.mult)
            stt("dve", rstd, ps[:, 5, :], 1.0 / D, nmr, ALU.mult, ALU.subtract)
            act(rstd, rstd, AF.Sqrt, bias=sm[:, 33:34])
            recip(rstd, rstd)
            stt("dve", nmr, mean, -1.0, rstd, ALU.mult, ALU.mult)
            gcol0 = PP_LN + lncol * 32
            for cb in range(16):
                tt("pool", zT[:, cb, :], zT[:, cb, :], rstd, ALU.mult)
                tt("dve", zT[:, cb, :], zT[:, cb, :], nmr, ALU.add)
                g_ = pp[:, l, gcol0 + cb:gcol0 + cb + 1]
                b_ = pp[:, l, gcol0 + 16 + cb:gcol0 + 16 + cb + 1]
                if final is None:
                    act(hT[:, cb, tsl], zT[:, cb, :], AF.Identity, bias=b_, scale=g_)
                else:
                    act(zT[:, cb, :], zT[:, cb, :], AF.Identity, bias=b_, scale=g_)
            if final is not None:
                s = final
                for tb4 in range(4):
                    slot = state["os"] % 2
                    state["os"] += 1
                    for g in range(4):
                        b = gbank()
                        for j in range(4):
                            cb = g * 4 + j
                            tr(ps[:, b, j * 128:(j + 1) * 128], zT[:, cb, tb4 * 128:(tb4 + 1) * 128])
                        evac(ostage[:, slot, g * 512:(g + 1) * 512], ps[:, b, :])
                    r0 = t_ * TT + tb4 * 128
                    dst = out_d[s, r0:r0 + 128, :]
                    src = ostage[:, slot, :]
                    P.add("sp", lambda e, d=dst, s_=src: e.dma_start(out=d, in_=s_), reads=[src], dma="os%d" % slot)

        def std_slabs(w_d, l):
            cache = {}

            def fn(cb):
                sl = cb // 2
                if sl not in cache:
                    cache.clear()
                    cache[sl] = load_w(w_d[l, sl], 16, 256)
                return [(cache[sl], [(kc, kc) for kc in range(16)], (cb % 2) * 128)]
            return fn

        def dn_slabs(l):
            def fn(cb):
                for half in range(2):
                    wv_ = load_w(wdn_d[l, cb, half], 22, 128)
                    yield (wv_, [(j, half * 22 + j) for j in range(22)], 0)
            return fn

        def mixer_out(l):
            for t_ in range(NTT):
                second_gemm_ln(l, t_, lambda kc, t_=t_: oT[:, kc, t_ * TT:(t_ + 1) * TT], 16, std_slabs(wout_d, l), 0)

        def memattn(l, s):
            for mb in range(2):
                dst = xst[:, mb, :]
                src = mem_d[s, mb * 128:(mb + 1) * 128, :]
                P.add("sp", lambda e, d=dst, s_=src: e.dma_start(out=d, in_=s_), writes=[dst], dma="xs%d" % mb)
                for g in range(4):
                    b = gbank()
                    for j in range(4):
                        kc = g * 4 + j
                        tr(ps[:, b, j * 128:(j + 1) * 128], xst[:, mb, kc * 128:(kc + 1) * 128])
                    evac(memT[:, g * 4:(g + 1) * 4, mb * 128:(mb + 1) * 128],
                         ps[:, b, :].rearrange("p (j c) -> p j c", j=4))
            for i in range(8):
                wk = load_w(wkv_d[l, i], 16, 256)
                for c2 in range(2):
                    b = gbank()
                    for kc in range(16):
                        mm(ps[:, b, 0:256], wk[:, kc, c2 * 128:(c2 + 1) * 128], memT[:, kc, :], kc == 0, kc == 15)
                    evac(KmT[:, 2 * i + c2, :], ps[:, b, 0:256])
            for i in range(8):
                wv = load_w(wkv_d[l, 8 + i], 16, 256)
                for mb in range(2):
                    b = gbank()
                    for kc in range(16):
                        mm(ps[:, b, 0:256], memT[:, kc, mb * 128:(mb + 1) * 128], wv[:, kc, :], kc == 0, kc == 15)
                    evac(Vm[:, mb, i * 256:(i + 1) * 256], ps[:, b, 0:256])
            sc = float(512 ** -0.5)
            for t_ in range(NTT):
                tsl = slice(t_ * TT, (t_ + 1) * TT)
                for i in range(8):
                    wq = load_w(wq_d[l, i], 16, 256)
                    for c2 in range(2):
                        b = gbank()
                        for kc in range(16):
                            mm(ps[:, b, :], wq[:, kc, c2 * 128:(c2 + 1) * 128], hT[:, kc, tsl], kc == 0, kc == 15)
                        evac(qTm[:, 2 * i + c2, :], ps[:, b, :])
                for h in range(4):
                    for mb in range(2):
                        for c in range(4):
                            mm(ps[:, 2 + mb, :], KmT[:, 4 * h + c, mb * 128:(mb + 1) * 128], qTm[:, 4 * h + c, :], c == 0, c == 3)
                        act(pTb[:, mb, :], ps[:, 2 + mb, :], AF.Exp, scale=sc)
                    for mb in range(2):
                        mm(ps[:, 4, :], ones_b[:, :], pTb[:, mb, :], mb == 0, mb == 1)
                    rden = tmp[:, 2, 0:512]
                    recip(rden, ps[:, 4, :])
                    for c in range(4):
                        b = gbank()
                        for mb in range(2):
                            mm(ps[:, b, :], Vm[:, mb, (4 * h + c) * 128:(4 * h + c + 1) * 128], pTb[:, mb, :], mb == 0, mb == 1)
                        tt("dve", omT[:, 4 * h + c, :], ps[:, b, :], rden, ALU.mult)
                second_gemm_ln(l, t_, lambda kc: omT[:, kc, :], 16, std_slabs(wo_d, l), 1)

        def ffn(l, s, last):
            for t_ in range(NTT):
                tsl = slice(t_ * TT, (t_ + 1) * TT)
                for c in range(NFC):
                    wgu = load_w(wup_d[l, c].rearrange("p a k c -> p (a k) c"), 32, 128)
                    if True:
                        res = []
                        for (koff, ch, slot0) in ((0, c, 0), (16, NFC + c, 3)):
                            b = gbank()
                            for kc in range(16):
                                mm(ps[:, b, :], wgu[:, koff + kc, :], hT[:, kc, tsl], kc == 0, kc == 15)
                            u = tmp[:, slot0, 0:514]
                            a_ = tmp[:, slot0 + 1, 0:512]
                            if t_ == 0:
                                memset("pool", u[:, 0:2], 0.0)
                            else:
                                cp("pool", u[:, 0:2], halo[:, ch, :])
                            cp("act", u[:, 2:514], ps[:, b, :])
                            cp("pool", halo[:, ch, :], u[:, 512:514])
                            w0 = pp[:, l, PP_CW + ch:PP_CW + ch + 1]
                            w1 = pp[:, l, PP_CW + 88 + ch:PP_CW + 88 + ch + 1]
                            w2 = pp[:, l, PP_CW + 176 + ch:PP_CW + 176 + ch + 1]
                            cb_ = pp[:, l, PP_CB + ch:PP_CB + ch + 1]
                            act(a_, u[:, 2:514], AF.Identity, bias=cb_, scale=w2)
                            stt("dve", a_, u[:, 1:513], w1, a_, ALU.mult, ALU.add)
                            stt("dve", a_, u[:, 0:512], w0, a_, ALU.mult, ALU.add)
                            res.append(a_)
                        sg = tmp[:, 2, 0:512]
                        act(sg, res[0], AF.Silu)
                        tt("dve", gT[:, c, :], sg, res[1], ALU.mult)
                second_gemm_ln(l, t_, lambda kc: gT[:, kc, :], NFC, dn_slabs(l), 2, final=(s if last else None))

        memset("pool", sm[:, 32:33], float(RMS_EPS))
        memset("pool", sm[:, 33:34], float(LN_EPS / (ALPHA * ALPHA)))

        dbg1 = nc.dram_tensor("dbg1", [128, 16, S], BF16, kind="ExternalOutput").ap()
        dbg2 = nc.dram_tensor("dbg2", [128, 16, S], BF16, kind="ExternalOutput").ap()
        dbg3 = nc.dram_tensor("dbg3", [128, 16, S], BF16, kind="ExternalOutput").ap()

        def main_gen():
          done = False
          for s in range(n_seq):
            load_x(s)
            for l in range(n_layers):
                last = (l == n_layers - 1)
                mixer(l)
                mixer_out(l)
                if s == 0 and l == 0 and stop_after is None:
                    P.add("sp", lambda e: e.dma_start(out=dbg1[:, :, :], in_=hT[:, :, :]), reads=[hT[:, :, :]], dma="dbg1")
                if stop_after == "mixer":
                    done = True
                    break
                memattn(l, s)
                if s == 0 and l == 0 and stop_after is None:
                    P.add("sp", lambda e: e.dma_start(out=dbg2[:, :, :], in_=hT[:, :, :]), reads=[hT[:, :, :]], dma="dbg2")
                if stop_after == "mem":
                    done = True
                    break
                ffn(l, s, last)
                if s == 0 and l == 0 and stop_after is None:
                    P.add("sp", lambda e: e.dma_start(out=dbg3[:, :, :], in_=hT[:, :, :]), reads=[hT[:, :, :]], dma="dbg3")
            if done:
                break

        P.dry = True
        main_gen()
        P.dry = False
        for k_ in state:
            state[k_] = 0
        main_gen()

        dbg = None
        if stop_after is not None:
            dbg = nc.dram_tensor("dbg", [128, 16, S], BF16, kind="ExternalOutput").ap()
            P.add("sp", lambda e: e.dma_start(out=dbg[:, :, :], in_=hT[:, :, :]), reads=[hT[:, :, :]], dma="dbg")

        fin_reads = [ostage[:, 0, :], ostage[:, 1, :]]
        P.add("sp", lambda e: e.nop(), reads=fin_reads + ([hT[:, :, :]] if dbg is not None else []),
              writes=fin_reads + ([hT[:, 0, 0:1]] if dbg is not None else []))

        with nc.Block() as block:
            P.emit(nc, block)
    return nc, P


def _consts():
    c = np.zeros((128, NCF), np.float32)
    c[:, C_ID:C_ID + 128] = np.eye(128, dtype=np.float32)
    ki = np.arange(128)[:, None]
    qi = np.arange(512)[None, :]
    c[:, C_DG:C_DG + 512] = (qi - ki).astype(np.float32)
    dd = (qi - ki).astype(np.float32)
    dd[dd < 0] = BIG
    c[:, C_DD:C_DD + 512] = dd
    j = np.arange(1152)[None, :]
    d = j - ki
    cm = ((d >= 0) & (d <= 128)).astype(np.float32) + ((d >= 0) & (d % 4 == 0) & (d <= 512)).astype(np.float32) \
        + ((d >= 0) & (d % 16 == 0)).astype(np.float32)
    c[:, C_CM:C_CM + 1152] = cm
    c[:, C_ON:C_ON + 128] = 1.0
    return c


def _slab(w, ncols):
    L, K, N = w.shape
    return np.ascontiguousarray(w.reshape(L, K // 128, 128, N // ncols, ncols).transpose(0, 3, 2, 1, 4))


def _prep(inputs):
    f = lambda k: np.asarray(inputs[k], dtype=np.float32)
    L = DEPTH
    shared = {}
    shared["w_in"] = _slab(f("w_in"), 256)
    shared["w_out"] = _slab(f("w_mix_out"), 256)
    shared["w_q"] = _slab(f("w_mem_q"), 256)
    shared["w_kv"] = _slab(f("w_mem_kv"), 256)
    shared["w_o"] = _slab(f("w_mem_o"), 256)
    wu = f("w_up")
    shared["w_up"] = np.ascontiguousarray(wu.reshape(L, 16, 128, 2, NFC, 128).transpose(0, 4, 2, 3, 1, 5))
    wd = f("w_down")
    shared["w_dn"] = np.ascontiguousarray(wd.reshape(L, 2, 22, 128, 16, 128).transpose(0, 4, 1, 3, 2, 5))
    pp = np.zeros((128, L, NP), np.float32)
    for l in range(L):
        for i, k in enumerate(["ln1_g", "ln1_b", "ln2_g", "ln2_b", "ln3_g", "ln3_b"]):
            pp[:, l, PP_LN + 16 * i:PP_LN + 16 * (i + 1)] = f(k)[l].reshape(16, 128).T
        cw = f("conv_w")[l]
        for k in range(3):
            pp[:, l, PP_CW + 88 * k:PP_CW + 88 * (k + 1)] = cw[k].reshape(88, 128).T
        pp[:, l, PP_CB:PP_CB + 88] = f("conv_b")[l].reshape(88, 128).T
        pp[:, l, PP_GD] = f("g_diff")[l]
        pp[:, l, PP_GL] = f("g_dil")[l]
        for i, k in enumerate(["lambda_q1", "lambda_k1", "lambda_q2", "lambda_k2"]):
            pp[:, l, PP_LAM + 64 * i:PP_LAM + 64 * (i + 1)] = f(k)[l][None, :]
    shared["pp"] = pp
    shared["cst"] = _consts()
    x = f("x")
    mem = f("mem")
    in_maps = []
    for c in range(NCORES):
        m = dict(shared)
        m["x"] = np.ascontiguousarray(x[c * SPC:(c + 1) * SPC])
        m["mem"] = np.ascontiguousarray(mem[c * SPC:(c + 1) * SPC])
        in_maps.append(m)
    return in_maps


_CACHE = {}


def kernel(**inputs):
    in_maps = _prep(inputs)
    if "nc" not in _CACHE:
        _CACHE["nc"] = build_program()[0]
    nc = _CACHE["nc"]
    res = run_bass_kernel_spmd(nc, in_maps, core_ids=list(range(NCORES)))
    _CACHE["last"] = res.results
    out = np.concatenate([np.asarray(r["out"], dtype=np.float32) for r in res.results], axis=0)
    return out
```

```python
import math
import numpy as np
import concourse.bass as bass
import concourse.mybir as mybir
from concourse.bass_utils import run_bass_kernel_spmd

F32 = mybir.dt.float32
BF16 = mybir.dt.bfloat16
AF = mybir.ActivationFunctionType
ALU = mybir.AluOpType
AX = mybir.AxisListType

D = 2048
S = 2048
NB = 16
NCORES = 8
SPC = NB // NCORES
DEPTH = 2
MEM = 256
DFF = 5632
NFC = DFF // 128
ALPHA = (2 * DEPTH) ** 0.25
INV_ALPHA = 1.0 / ALPHA
LN_EPS = 1e-5
RMS_EPS = 1e-5
BIG = 1.0e7
TT = 512
NTT = S // TT

_sl = (2.0 ** (-8.0 * np.arange(1, 17) / 16)).astype(np.float32)
SLOPES_DIFF = [float(v) for v in _sl[0::2]]
SLOPES_DIL = [float(v) for v in _sl[1::2]]

PP_LN = 0
PP_CW = 96
PP_CB = PP_CW + 3 * 88
PP_GD = PP_CB + 88
PP_GL = PP_GD + 1
PP_LAM = PP_GL + 1
NP = PP_LAM + 256
C_ID = 0
C_DG = 128
C_DD = C_DG + 512
C_CM = C_DD + 512
C_ON = C_CM + 1152
NCF = C_ON + 128

_ESZ = {F32: 4, BF16: 2}
GRAN = 256


class _Op:
    __slots__ = ("eng", "fn", "deps", "dma", "sig", "signo", "dcount")

    def __init__(self, eng, fn, dma):
        self.eng = eng
        self.fn = fn
        self.deps = ()
        self.dma = dma
        self.sig = False
        self.signo = 0
        self.dcount = 0


class Prog:
    CAP = 30000

    def __init__(self):
        self.ops = []
        self.lastw = {}
        self.readers = {}
        self.nsync = True
        self.dry = False

    def _grans(self, ap):
        t = ap.tensor
        esz = _ESZ[ap.dtype]
        pat = ap.ap
        pstride = pat[0][0]
        off = ap.offset % pstride if pstride > 0 else ap.offset
        ext = 1
        for st, cnt in pat[1:]:
            ext += (cnt - 1) * abs(st)
        b0 = off * esz
        b1 = (off + ext) * esz
        name = t.name
        return [(name, g) for g in range(b0 // GRAN, (b1 - 1) // GRAN + 1)]

    def add(self, eng, fn, reads=(), writes=(), dma=None):
        if self.dry:
            return -1
        op = _Op(eng, fn, dma)
        idx = len(self.ops)
        ops = self.ops
        lastw = self.lastw
        readers = self.readers
        mykey = ("d", dma) if dma is not None else ("e", eng)
        best = {}

        def dep(d):
            o = ops[d]
            k = ("d", o.dma) if o.dma is not None else ("e", o.eng)
            if best.get(k, -1) < d:
                best[k] = d

        rg = []
        for ap in reads:
            rg.extend(self._grans(ap))
        wg = []
        for ap in writes:
            wg.extend(self._grans(ap))
        for g in rg:
            w = lastw.get(g)
            if w is not None:
                dep(w)
        for g in wg:
            w = lastw.get(g)
            if w is not None:
                dep(w)
            r = readers.get(g)
            if r:
                for d in r.values():
                    dep(d)
        for g in rg:
            r = readers.get(g)
            if r is None:
                readers[g] = {mykey: idx}
            else:
                r[mykey] = idx
        for g in wg:
            lastw[g] = idx
            readers[g] = None
        fdeps = []
        for k, d in best.items():
            o = ops[d]
            if o.dma is None and o.eng == eng and dma is None:
                if eng == "pe" or not self.nsync:
                    continue
            o.sig = True
            fdeps.append(d)
        op.deps = fdeps
        ops.append(op)
        return idx

    def emit(self, nc, block):
        ops = self.ops
        ecount = {}
        dcount = {}
        for op in ops:
            if op.dma is not None:
                dcount[op.dma] = dcount.get(op.dma, 0) + 1
                op.dcount = dcount[op.dma]
            elif op.sig:
                ecount[op.eng] = ecount.get(op.eng, 0) + 1
                op.signo = ecount[op.eng]
        CAP = self.CAP
        DCAP = CAP // 16
        sems = {}

        def sem_for(key):
            s = sems.get(key)
            if s is None:
                s = nc.alloc_semaphore(name="s_%s_%d" % (str(key[0]), key[1]))
                sems[key] = s
            return s

        def sem_of(op):
            if op.dma is not None:
                n = op.dcount - 1
                return ("d_" + op.dma, n // DCAP), (n % DCAP + 1) * 16
            n = op.signo - 1
            return (op.eng, n // CAP), n % CAP + 1

        for op in ops:
            if op.dma is not None or op.sig:
                sem_for(sem_of(op)[0])
        per_eng = {"pe": [], "act": [], "dve": [], "pool": [], "sp": []}
        for op in ops:
            per_eng[op.eng].append(op)

        def run(e, lst):
            waited = {}
            for op in lst:
                need = {}
                for d in op.deps:
                    k, v = sem_of(ops[d])
                    if need.get(k, 0) < v:
                        need[k] = v
                for k, v in need.items():
                    if waited.get(k, 0) >= v:
                        continue
                    e.wait_ge(sems[k], v)
                    waited[k] = v
                ins = op.fn(e)
                if op.dma is not None:
                    ins.then_inc(sems[sem_of(op)[0]], 16)
                elif op.sig:
                    ins.then_inc(sems[sem_of(op)[0]], 1)

        @block.tensor
        def _(e):
            run(e, per_eng["pe"])

        @block.scalar
        def _(e):
            run(e, per_eng["act"])

        @block.vector
        def _(e):
            run(e, per_eng["dve"])

        @block.gpsimd
        def _(e):
            run(e, per_eng["pool"])

        @block.sync
        def _(e):
            run(e, per_eng["sp"])

        return sems


def build_program(n_seq=SPC, n_layers=DEPTH, stop_after=None, nsync=True):
    nc = bass.Bass("TRN2", target_bir_lowering=False)
    L = DEPTH
    x_d = nc.dram_tensor("x", [SPC, S, D], F32, kind="ExternalInput").ap()
    mem_d = nc.dram_tensor("mem", [SPC, MEM, D], F32, kind="ExternalInput").ap()
    win_d = nc.dram_tensor("w_in", [L, 24, 128, 16, 256], F32, kind="ExternalInput").ap()
    wout_d = nc.dram_tensor("w_out", [L, 8, 128, 16, 256], F32, kind="ExternalInput").ap()
    wq_d = nc.dram_tensor("w_q", [L, 8, 128, 16, 256], F32, kind="ExternalInput").ap()
    wkv_d = nc.dram_tensor("w_kv", [L, 16, 128, 16, 256], F32, kind="ExternalInput").ap()
    wo_d = nc.dram_tensor("w_o", [L, 8, 128, 16, 256], F32, kind="ExternalInput").ap()
    wup_d = nc.dram_tensor("w_up", [L, 44, 128, 2, 16, 128], F32, kind="ExternalInput").ap()
    wdn_d = nc.dram_tensor("w_dn", [L, 16, 2, 128, 22, 128], F32, kind="ExternalInput").ap()
    pp_d = nc.dram_tensor("pp", [128, L, NP], F32, kind="ExternalInput").ap()
    cst_d = nc.dram_tensor("cst", [128, NCF], F32, kind="ExternalInput").ap()
    out_d = nc.dram_tensor("out", [SPC, S, D], F32, kind="ExternalOutput").ap()

    P = Prog()
    P.nsync = nsync
    wreqs = []

    import contextlib
    stack = contextlib.ExitStack()
    with stack:
        def SB(name, shape, dt):
            return stack.enter_context(nc.sbuf_tensor("sb_" + name, shape, dt))

        hT = SB("hT", [128, 16, S], BF16)
        aT = SB("aT", [128, 32768], BF16)
        wb = [SB("wb%d" % i, [128, 4096], BF16) for i in range(2)]
        scr = SB("scr", [128, 8192], F32)
        tmp = SB("tmp", [128, 6, 516], F32)
        pTb = SB("pT", [128, 4, 512], BF16)
        cf = SB("cf", [128, C_CM], F32)
        cmask = SB("cmask", [128, 1152], BF16)
        ones_b = SB("ones_b", [128, 128], BF16)
        pp = SB("pp", [128, L, NP], F32)
        halo = SB("halo", [128, 88, 2], F32)
        sm = SB("sm", [128, 64], F32)
        ps = stack.enter_context(nc.psum_tensor("ps", [128, 8, 512], F32))

        ident = cf[:, C_ID:C_ID + 128]
        distg = cf[:, C_DG:C_DG + 512]
        distd = cf[:, C_DD:C_DD + 512]

        oT = aT[:, :].rearrange("p (k t) -> p k t", k=16)
        scr_b = scr[:, :].bitcast(BF16)
        qT2 = scr_b[:, 0:4096].rearrange("p (h t) -> p h t", h=2)
        kT2 = scr_b[:, 4096:8192].rearrange("p (h t) -> p h t", h=2)
        vA = scr_b[:, 8192:8192 + 16 * 2 * 130].rearrange("p (b h c) -> p b h c", b=16, h=2)
        zT = scr[:, :].rearrange("p (k t) -> p k t", k=16)
        xst = scr[:, :].rearrange("p (i c) -> p i c", i=4)

        qTm = aT[:, 0:8192].rearrange("p (k t) -> p k t", k=16)
        omT = aT[:, 8192:16384].rearrange("p (k t) -> p k t", k=16)
        memT = aT[:, 16384:20480].rearrange("p (k t) -> p k t", k=16)
        KmT = aT[:, 20480:24576].rearrange("p (k t) -> p k t", k=16)
        Vm = aT[:, 24576:28672].rearrange("p (b c) -> p b c", b=2)
        gT = aT[:, 0:NFC * 512].rearrange("p (k t) -> p k t", k=NFC)
        ostage = aT[:, NFC * 512:NFC * 512 + 8192].bitcast(F32).rearrange("p (i c) -> p i c", i=2)

        state = {"w": 0, "bank": 0, "sbank": 0, "abank": 0, "ev": 0, "x": 0, "os": 0}

        def _issue_w(k):
            src_ap, nk, ncols = wreqs[k]
            i = k % 2
            dst = wb[i][:, 0:nk * ncols].rearrange("p (k c) -> p k c", k=nk)
            P.add("pool", lambda e, d=dst, s=src_ap: e.dma_start(out=d, in_=s), writes=[dst], dma="w%d" % i)

        def load_w(src_ap, nk, ncols):
            k = state["w"]
            state["w"] += 1
            if P.dry:
                wreqs.append((src_ap, nk, ncols))
            else:
                if k == 0:
                    _issue_w(0)
                if k + 1 < len(wreqs):
                    _issue_w(k + 1)
            i = k % 2
            return wb[i][:, 0:nk * ncols].rearrange("p (k c) -> p k c", k=nk)

        def gbank():
            b = state["bank"] % 2
            state["bank"] += 1
            return b

        def mm(out, lhsT, rhs, start, stop, skip=False):
            P.add("pe", lambda e, o=out, l=lhsT, r=rhs, s0=start, s1=stop, sk=skip: e.matmul(o, lhsT=l, rhs=r, start=s0, stop=s1, skip_group_check=sk),
                  reads=[lhsT, rhs], writes=[out])

        def tr(out, in_):
            P.add("pe", lambda e, o=out, i=in_: e.transpose(o, i, ident), reads=[in_, ident], writes=[out])

        def act(out, in_, func, bias=0.0, scale=1.0, extra_reads=()):
            rd = [in_] + list(extra_reads)
            if not isinstance(bias, float):
                rd.append(bias)
            if not isinstance(scale, float):
                rd.append(scale)
            P.add("act", lambda e, o=out, i=in_, f=func, b=bias, s=scale: e.activation(out=o, in_=i, func=f, bias=b, scale=s),
                  reads=rd, writes=[out])

        def ts(eng, out, in0, s1, s2, op0, op1=None):
            rd = [in0]
            if not isinstance(s1, float):
                rd.append(s1)
            if s2 is not None and not isinstance(s2, float):
                rd.append(s2)
            if op1 is None:
                P.add(eng, lambda e, o=out, i=in0, a=s1, p0=op0: e.tensor_scalar(out=o, in0=i, scalar1=a, scalar2=None, op0=p0),
                      reads=rd, writes=[out])
            else:
                P.add(eng, lambda e, o=out, i=in0, a=s1, b=s2, p0=op0, p1=op1: e.tensor_scalar(out=o, in0=i, scalar1=a, scalar2=b, op0=p0, op1=p1),
                      reads=rd, writes=[out])

        def stt(eng, out, in0, scalar, in1, op0, op1):
            rd = [in0, in1]
            if not isinstance(scalar, float):
                rd.append(scalar)
            P.add(eng, lambda e, o=out, i0=in0, s=scalar, i1=in1, p0=op0, p1=op1: e.scalar_tensor_tensor(out=o, in0=i0, scalar=s, in1=i1, op0=p0, op1=p1),
                  reads=rd, writes=[out])

        def tt(eng, out, in0, in1, op):
            P.add(eng, lambda e, o=out, i0=in0, i1=in1, p=op: e.tensor_tensor(out=o, in0=i0, in1=i1, op=p),
                  reads=[in0, in1], writes=[out])

        def cp(eng, out, in_):
            if eng == "act":
                P.add("act", lambda e, o=out, i=in_: e.copy(out=o, in_=i), reads=[in_], writes=[out])
            else:
                P.add(eng, lambda e, o=out, i=in_: e.tensor_copy(out=o, in_=i), reads=[in_], writes=[out])

        def evac(out, in_, scale=None):
            k = state["ev"]
            state["ev"] += 1
            if k % 2 == 0:
                if scale is None:
                    cp("act", out, in_)
                else:
                    P.add("act", lambda e, o=out, i=in_, s=scale: e.mul(out=o, in_=i, mul=s), reads=[in_], writes=[out])
            else:
                if scale is None:
                    cp("dve", out, in_)
                else:
                    ts("dve", out, in_, scale, None, ALU.mult)

        def memset(eng, ap, val):
            P.add(eng, lambda e, a=ap, v=val: e.memset(a, v), writes=[ap])

        def recip(out, in_):
            P.add("dve", lambda e, o=out, i=in_: e.reciprocal(out=o, in_=i), reads=[in_], writes=[out])

        P.add("sp", lambda e: e.dma_start(out=cf[:, :], in_=cst_d[:, 0:C_CM]), writes=[cf[:, :]], dma="c0")
        P.add("pool", lambda e: e.dma_start(out=cmask[:, :], in_=cst_d[:, C_CM:C_CM + 1152]), writes=[cmask[:, :]], dma="c1")
        P.add("pool", lambda e: e.dma_start(out=ones_b[:, :], in_=cst_d[:, C_ON:C_ON + 128]), writes=[ones_b[:, :]], dma="c2")
        P.add("sp", lambda e: e.dma_start(out=pp[:, :, :], in_=pp_d[:, :, :]), writes=[pp[:, :, :]], dma="c3")

        for l in range(L):
            lam_init = 0.8 - 0.6 * math.exp(-0.3 * l)
            lq1 = pp[:, l, PP_LAM:PP_LAM + 64]
            lk1 = pp[:, l, PP_LAM + 64:PP_LAM + 128]
            lq2 = pp[:, l, PP_LAM + 128:PP_LAM + 192]
            lk2 = pp[:, l, PP_LAM + 192:PP_LAM + 256]
            t0 = tmp[:, 0, 0:64]
            t1 = tmp[:, 1, 0:64]
            tt("dve", t0, lq1, lk1, ALU.mult)
            P.add("dve", lambda e, o=sm[:, 8:9], i=t0: e.reduce_sum(out=o, in_=i, axis=AX.X), reads=[t0], writes=[sm[:, 8:9]])
            tt("dve", t1, lq2, lk2, ALU.mult)
            P.add("dve", lambda e, o=sm[:, 9:10], i=t1: e.reduce_sum(out=o, in_=i, axis=AX.X), reads=[t1], writes=[sm[:, 9:10]])
            act(sm[:, 10:12], sm[:, 8:10], AF.Exp)
            stt("dve", sm[:, l:l + 1], sm[:, 10:11], 1.0, sm[:, 11:12], ALU.mult, ALU.subtract)
            ts("dve", sm[:, l:l + 1], sm[:, l:l + 1], float(lam_init), None, ALU.add)
            ts("dve", sm[:, 2 + l:3 + l], sm[:, l:l + 1], -1.0, None, ALU.mult)
            ts("dve", sm[:, 4 + l:5 + l], pp[:, l, PP_GD:PP_GD + 1], float(1.0 - lam_init), None, ALU.mult)

        def load_x(s):
            for tb in range(16):
                slot = tb % 4
                dst = xst[:, slot, :]
                src = x_d[s, tb * 128:(tb + 1) * 128, :]
                P.add("sp", lambda e, d=dst, s_=src: e.dma_start(out=d, in_=s_), writes=[dst], dma="xs%d" % slot)
                for g in range(4):
                    b = gbank()
                    for j in range(4):
                        kc = g * 4 + j
                        tr(ps[:, b, j * 128:(j + 1) * 128], xst[:, slot, kc * 128:(kc + 1) * 128])
                    evac(hT[:, g * 4:(g + 1) * 4, tb * 128:(tb + 1) * 128],
                         ps[:, b, :].rearrange("p (j c) -> p j c", j=4))

        def attn_qtile(qt, lo, hi, hh, slope, dil, pending=None):
            q0 = qt * 512
            nkb = 4 * qt + 4
            LA = 2
            accv = ps[:, 4:6, :].rearrange("p b (h c) -> p (b h) c", h=2)
            info = []

            def stage1(kb):
                j = kb - 4 * qt
                if j >= 0:
                    c0 = 128 * j
                    n = 512 - c0
                    dist = distd[:, 0:n]
                    bias = 0.0
                    cm = cmask[:, 0:n]
                else:
                    delta = q0 - kb * 128
                    c0 = 0
                    n = 512
                    dist = distg
                    bias = float(-slope * delta)
                    cmo = min(delta, 640)
                    cm = cmask[:, cmo:cmo + 512]
                t = state["sbank"]
                state["sbank"] += 1
                sbk = (2, 3, 6, 7)[t % 4]
                xi = (0, 1, 5)[t % 3]
                pi = t % 4
                pss = ps[:, sbk, 0:n]
                mm(pss, kT2[lo:hi, hh, kb * 128:(kb + 1) * 128], qT2[lo:hi, hh, q0 + c0:q0 + 512], True, True)
                xs_ = tmp[:, xi, 0:n]
                stt("dve", xs_, dist, float(-slope), pss, ALU.mult, ALU.add)
                pt = pTb[:, pi, 0:n]
                act(pt, xs_, AF.Exp, bias=bias)
                if dil:
                    tt("pool", pt, pt, cm, ALU.mult)
                info.append((kb, j, c0, pi))

            def stage2(kb, j, c0, pi):
                for s_ in range(max(j, 0), 4):
                    col = s_ * 128 - c0
                    mm(accv[:, s_, 0:129], pTb[:, pi, col:col + 128], vA[:, kb, hh, 0:129],
                       kb == 0 and s_ % 2 == 0, kb == 4 * qt + s_, skip=True)

            for t in range(nkb + LA):
                if t < nkb:
                    stage1(t)
                if t == LA - 1 and pending is not None:
                    pending()
                if t >= LA:
                    stage2(*info[t - LA])
            return accv

        def attn_finish(accv, qt, chunk, gcol, first_map=False, second_map=False, lamneg=None):
            q0 = qt * 512
            rec = sm[:, 16:20]
            osb = tmp[:, 2, 0:512].rearrange("p (s c) -> p s c", s=4)
            o0 = tmp[:, 3, 0:512].rearrange("p (s c) -> p s c", s=4)
            sq = tmp[:, 4, 0:512].rearrange("p (s c) -> p s c", s=4)
            recip(rec, accv[:, :, 128:129].rearrange("p s c -> p (s c)"))
            if first_map:
                for s_ in range(4):
                    ts("dve", o0[:, s_, :], accv[:, s_, 0:128], rec[:, s_:s_ + 1], None, ALU.mult)
                return None
            if second_map:
                ts("dve", rec, rec, lamneg, None, ALU.mult)
                for s_ in range(4):
                    stt("dve", osb[:, s_, :], accv[:, s_, 0:128], rec[:, s_:s_ + 1], o0[:, s_, :], ALU.mult, ALU.add)
            else:
                for s_ in range(4):
                    ts("dve", osb[:, s_, :], accv[:, s_, 0:128], rec[:, s_:s_ + 1], None, ALU.mult)
            tt("pool", sq, osb, osb, ALU.mult)
            ss = sm[:, 20:24]
            P.add("dve", lambda e, o=ss, i=sq: e.reduce_sum(out=o, in_=i, axis=AX.X), reads=[sq], writes=[ss])
            sd = sm[:, 24:28]
            act(sd, ss, AF.Sqrt, bias=sm[:, 32:33], scale=1.0 / 128.0)
            rstd = sm[:, 28:32]
            recip(rstd, sd)
            for s_ in range(4):
                ts("dve", osb[:, s_, :], osb[:, s_, :], rstd[:, s_:s_ + 1], None, ALU.mult)
            def part2():
                b = gbank()
                for s_ in range(4):
                    tr(ps[:, b, s_ * 128:(s_ + 1) * 128], osb[:, s_, :])
                act(oT[:, chunk, q0:q0 + 512], ps[:, b, :], AF.Copy, scale=gcol)
            return part2

        def mixer(l):
            memset("pool", vA[:, :, :, 128:130], 1.0)
            for pair in range(8):
                dil = pair >= 4
                jp = pair - 4 if dil else pair
                base = 12 if dil else 0
                scale = (128 ** -0.5) if dil else (64 ** -0.5)
                wq = load_w(win_d[l, base + jp], 16, 256)
                for hh in range(2):
                    for t_ in range(NTT):
                        b = gbank()
                        for kc in range(16):
                            mm(ps[:, b, :], wq[:, kc, hh * 128:(hh + 1) * 128], hT[:, kc, t_ * TT:(t_ + 1) * TT], kc == 0, kc == 15)
                        evac(qT2[:, hh, t_ * TT:(t_ + 1) * TT], ps[:, b, :], scale=float(scale))
                wk = load_w(win_d[l, base + 4 + jp], 16, 256)
                for hh in range(2):
                    for t_ in range(NTT):
                        b = gbank()
                        for kc in range(16):
                            mm(ps[:, b, :], wk[:, kc, hh * 128:(hh + 1) * 128], hT[:, kc, t_ * TT:(t_ + 1) * TT], kc == 0, kc == 15)
                        evac(kT2[:, hh, t_ * TT:(t_ + 1) * TT], ps[:, b, :])
                wv = load_w(win_d[l, base + 8 + jp], 16, 256)
                for tb in range(16):
                    b = gbank()
                    for kc in range(16):
                        mm(ps[:, b, 0:256], hT[:, kc, tb * 128:(tb + 1) * 128], wv[:, kc, :], kc == 0, kc == 15)
                    evac(vA[:, tb, :, 0:128], ps[:, b, 0:256].rearrange("p (h c) -> p h c", h=2))
                pend = None
                for hh in range(2):
                    head = jp * 2 + hh
                    if dil:
                        slope = SLOPES_DIL[head]
                        chunk = 8 + head
                        for qt in range(NTT):
                            accv = attn_qtile(qt, 0, 128, hh, slope, True, pend)
                            pend = attn_finish(accv, qt, chunk, pp[:, l, PP_GL:PP_GL + 1])
                    else:
                        slope = SLOPES_DIFF[head]
                        chunk = head
                        for qt in range(NTT):
                            accv = attn_qtile(qt, 0, 64, hh, slope, False, pend)
                            attn_finish(accv, qt, chunk, None, first_map=True)
                            accv = attn_qtile(qt, 64, 128, hh, slope, False, None)
                            pend = attn_finish(accv, qt, chunk, sm[:, 4 + l:5 + l], second_map=True, lamneg=sm[:, 2 + l:3 + l])
                if pend is not None:
                    pend()

        def second_gemm_ln(l, t_, rhs_fn, nk, slab_fn, lncol, final=None):
            tsl = slice(t_ * TT, (t_ + 1) * TT)
            pend_stats = None
            for cb in range(16):
                b = gbank()
                k = 0
                for (wv_, kcs, coff) in slab_fn(cb):
                    for (wi, kc) in kcs:
                        mm(ps[:, b, :], wv_[:, wi, coff:coff + 128], rhs_fn(kc), k == 0, k == nk - 1)
                        k += 1
                assert k == nk
                if pend_stats is not None:
                    pend_stats()
                stt("dve", zT[:, cb, :], ps[:, b, :], float(INV_ALPHA), hT[:, cb, tsl], ALU.mult, ALU.add)
                zb = pTb[:, (2 * cb) % 4, :]
                zq = pTb[:, (2 * cb + 1) % 4, :]
                cp("act", zb, zT[:, cb, :])
                act(zq, zT[:, cb, :], AF.Square)

                def mk(cb=cb, zb=zb, zq=zq):
                    def f():
                        mm(ps[:, 4, :], ones_b[:, :], zb, cb == 0, cb == 15)
                        mm(ps[:, 5, :], ones_b[:, :], zq, cb == 0, cb == 15)
                    return f
                pend_stats = mk()
            pend_stats()
            mean = tmp[:, 0, 0:512]
            rstd = tmp[:, 1, 0:512]
            nmr = tmp[:, 5, 0:512]
            ts("dve", mean, ps[:, 4, :], 1.0 / D, None, ALU.mult)
            tt("dve", nmr, mean, mean, ALU.mult)
            stt("dve", rstd, ps[:, 5, :], 1.0 / D, nmr, ALU.mult, ALU.subtract)
            act(rstd, rstd, AF.Sqrt, bias=sm[:, 33:34])
            recip(rstd, rstd)
            stt("dve", nmr, mean, -1.0, rstd, ALU.mult, ALU.mult)
            gcol0 = PP_LN + lncol * 32
            for cb in range(16):
                tt("pool", zT[:, cb, :], zT[:, cb, :], rstd, ALU.mult)
                tt("dve", zT[:, cb, :], zT[:, cb, :], nmr, ALU.add)
                g_ = pp[:, l, gcol0 + cb:gcol0 + cb + 1]
                b_ = pp[:, l, gcol0 + 16 + cb:gcol0 + 16 + cb + 1]
                if final is None:
                    act(hT[:, cb, tsl], zT[:, cb, :], AF.Identity, bias=b_, scale=g_)
                else:
                    act(zT[:, cb, :], zT[:, cb, :], AF.Identity, bias=b_, scale=g_)
            if final is not None:
                s = final
                for tb4 in range(4):
                    slot = state["os"] % 2
                    state["os"] += 1
                    for g in range(4):
                        b = gbank()
                        for j in range(4):
                            cb = g * 4 + j
                            tr(ps[:, b, j * 128:(j + 1) * 128], zT[:, cb, tb4 * 128:(tb4 + 1) * 128])
                        evac(ostage[:, slot, g * 512:(g + 1) * 512], ps[:, b, :])
                    r0 = t_ * TT + tb4 * 128
                    dst = out_d[s, r0:r0 + 128, :]
                    src = ostage[:, slot, :]
                    P.add("sp", lambda e, d=dst, s_=src: e.dma_start(out=d, in_=s_), reads=[src], dma="os%d" % slot)

        def std_slabs(w_d, l):
            cache = {}

            def fn(cb):
                sl = cb // 2
                if sl not in cache:
                    cache.clear()
                    cache[sl] = load_w(w_d[l, sl], 16, 256)
                return [(cache[sl], [(kc, kc) for kc in range(16)], (cb % 2) * 128)]
            return fn

        def dn_slabs(l):
            def fn(cb):
                for half in range(2):
                    wv_ = load_w(wdn_d[l, cb, half], 22, 128)
                    yield (wv_, [(j, half * 22 + j) for j in range(22)], 0)
            return fn

        def mixer_out(l):
            for t_ in range(NTT):
                second_gemm_ln(l, t_, lambda kc, t_=t_: oT[:, kc, t_ * TT:(t_ + 1) * TT], 16, std_slabs(wout_d, l), 0)

        def memattn(l, s):
            for mb in range(2):
                dst = xst[:, mb, :]
                src = mem_d[s, mb * 128:(mb + 1) * 128, :]
                P.add("sp", lambda e, d=dst, s_=src: e.dma_start(out=d, in_=s_), writes=[dst], dma="xs%d" % mb)
                for g in range(4):
                    b = gbank()
                    for j in range(4):
                        kc = g * 4 + j
                        tr(ps[:, b, j * 128:(j + 1) * 128], xst[:, mb, kc * 128:(kc + 1) * 128])
                    evac(memT[:, g * 4:(g + 1) * 4, mb * 128:(mb + 1) * 128],
                         ps[:, b, :].rearrange("p (j c) -> p j c", j=4))
            for i in range(8):
                wk = load_w(wkv_d[l, i], 16, 256)
                for c2 in range(2):
                    b = gbank()
                    for kc in range(16):
                        mm(ps[:, b, 0:256], wk[:, kc, c2 * 128:(c2 + 1) * 128], memT[:, kc, :], kc == 0, kc == 15)
                    evac(KmT[:, 2 * i + c2, :], ps[:, b, 0:256])
            for i in range(8):
                wv = load_w(wkv_d[l, 8 + i], 16, 256)
                for mb in range(2):
                    b = gbank()
                    for kc in range(16):
                        mm(ps[:, b, 0:256], memT[:, kc, mb * 128:(mb + 1) * 128], wv[:, kc, :], kc == 0, kc == 15)
                    evac(Vm[:, mb, i * 256:(i + 1) * 256], ps[:, b, 0:256])
            sc = float(512 ** -0.5)
            for t_ in range(NTT):
                tsl = slice(t_ * TT, (t_ + 1) * TT)
                for i in range(8):
                    wq = load_w(wq_d[l, i], 16, 256)
                    for c2 in range(2):
                        b = gbank()
                        for kc in range(16):
                            mm(ps[:, b, :], wq[:, kc, c2 * 128:(c2 + 1) * 128], hT[:, kc, tsl], kc == 0, kc == 15)
                        evac(qTm[:, 2 * i + c2, :], ps[:, b, :])
                def s_stage(h):
                    for mb in range(2):
                        sbk = (2, 3, 6, 7)[2 * (h % 2) + mb]
                        for c in range(4):
                            mm(ps[:, sbk, :], KmT[:, 4 * h + c, mb * 128:(mb + 1) * 128], qTm[:, 4 * h + c, :], c == 0, c == 3)
                        act(pTb[:, 2 * (h % 2) + mb, :], ps[:, sbk, :], AF.Exp, scale=sc)

                def pv_stage(h):
                    pb = 2 * (h % 2)
                    for mb in range(2):
                        mm(ps[:, 4, :], ones_b[:, :], pTb[:, pb + mb, :], mb == 0, mb == 1)
                    rden = tmp[:, 2 + (h % 2), 0:512]
                    recip(rden, ps[:, 4, :])
                    for c in range(4):
                        b = gbank()
                        for mb in range(2):
                            mm(ps[:, b, :], Vm[:, mb, (4 * h + c) * 128:(4 * h + c + 1) * 128], pTb[:, pb + mb, :], mb == 0, mb == 1)
                        tt("dve", omT[:, 4 * h + c, :], ps[:, b, :], rden, ALU.mult)

                s_stage(0)
                for h in range(4):
                    if h + 1 < 4:
                        s_stage(h + 1)
                    pv_stage(h)
                second_gemm_ln(l, t_, lambda kc: omT[:, kc, :], 16, std_slabs(wo_d, l), 1)

        def ffn(l, s, last):
            for t_ in range(NTT):
                tsl = slice(t_ * TT, (t_ + 1) * TT)
                for c in range(NFC):
                    wgu = load_w(wup_d[l, c].rearrange("p a k c -> p (a k) c"), 32, 128)
                    if True:
                        res = []
                        for (koff, ch, slot0) in ((0, c, 0), (16, NFC + c, 3)):
                            b = gbank()
                            for kc in range(16):
                                mm(ps[:, b, :], wgu[:, koff + kc, :], hT[:, kc, tsl], kc == 0, kc == 15)
                            u = tmp[:, slot0, 0:514]
                            a_ = tmp[:, slot0 + 1, 0:512]
                            if t_ == 0:
                                memset("pool", u[:, 0:2], 0.0)
                            else:
                                cp("pool", u[:, 0:2], halo[:, ch, :])
                            cp("act", u[:, 2:514], ps[:, b, :])
                            cp("pool", halo[:, ch, :], u[:, 512:514])
                            w0 = pp[:, l, PP_CW + ch:PP_CW + ch + 1]
                            w1 = pp[:, l, PP_CW + 88 + ch:PP_CW + 88 + ch + 1]
                            w2 = pp[:, l, PP_CW + 176 + ch:PP_CW + 176 + ch + 1]
                            cb_ = pp[:, l, PP_CB + ch:PP_CB + ch + 1]
                            act(a_, u[:, 2:514], AF.Identity, bias=cb_, scale=w2)
                            stt("dve", a_, u[:, 1:513], w1, a_, ALU.mult, ALU.add)
                            stt("dve", a_, u[:, 0:512], w0, a_, ALU.mult, ALU.add)
                            res.append(a_)
                        sg = tmp[:, 2, 0:512]
                        act(sg, res[0], AF.Silu)
                        tt("dve", gT[:, c, :], sg, res[1], ALU.mult)
                second_gemm_ln(l, t_, lambda kc: gT[:, kc, :], NFC, dn_slabs(l), 2, final=(s if last else None))

        memset("pool", sm[:, 32:33], float(RMS_EPS))
        memset("pool", sm[:, 33:34], float(LN_EPS / (ALPHA * ALPHA)))

        dbg1 = nc.dram_tensor("dbg1", [128, 16, S], BF16, kind="ExternalOutput").ap()
        dbg2 = nc.dram_tensor("dbg2", [128, 16, S], BF16, kind="ExternalOutput").ap()
        dbg3 = nc.dram_tensor("dbg3", [128, 16, S], BF16, kind="ExternalOutput").ap()

        def main_gen():
          done = False
          for s in range(n_seq):
            load_x(s)
            for l in range(n_layers):
                last = (l == n_layers - 1)
                mixer(l)
                mixer_out(l)
                if s == 0 and l == 0 and stop_after is None:
                    P.add("sp", lambda e: e.dma_start(out=dbg1[:, :, :], in_=hT[:, :, :]), reads=[hT[:, :, :]], dma="dbg1")
                if stop_after == "mixer":
                    done = True
                    break
                memattn(l, s)
                if s == 0 and l == 0 and stop_after is None:
                    P.add("sp", lambda e: e.dma_start(out=dbg2[:, :, :], in_=hT[:, :, :]), reads=[hT[:, :, :]], dma="dbg2")
                if stop_after == "mem":
                    done = True
                    break
                ffn(l, s, last)
                if s == 0 and l == 0 and stop_after is None:
                    P.add("sp", lambda e: e.dma_start(out=dbg3[:, :, :], in_=hT[:, :, :]), reads=[hT[:, :, :]], dma="dbg3")
            if done:
                break

        P.dry = True
        main_gen()
        P.dry = False
        for k_ in state:
            state[k_] = 0
        main_gen()

        dbg = None
        if stop_after is not None:
            dbg = nc.dram_tensor("dbg", [128, 16, S], BF16, kind="ExternalOutput").ap()
            P.add("sp", lambda e: e.dma_start(out=dbg[:, :, :], in_=hT[:, :, :]), reads=[hT[:, :, :]], dma="dbg")

        fin_reads = [ostage[:, 0, :], ostage[:, 1, :]]
        P.add("sp", lambda e: e.nop(), reads=fin_reads + ([hT[:, :, :]] if dbg is not None else []),
              writes=fin_reads + ([hT[:, 0, 0:1]] if dbg is not None else []))

        with nc.Block() as block:
            P.emit(nc, block)
    return nc, P


def _consts():
    c = np.zeros((128, NCF), np.float32)
    c[:, C_ID:C_ID + 128] = np.eye(128, dtype=np.float32)
    ki = np.arange(128)[:, None]
    qi = np.arange(512)[None, :]
    c[:, C_DG:C_DG + 512] = (qi - ki).astype(np.float32)
    dd = (qi - ki).astype(np.float32)
    dd[dd < 0] = BIG
    c[:, C_DD:C_DD + 512] = dd
    j = np.arange(1152)[None, :]
    d = j - ki
    cm = ((d >= 0) & (d <= 128)).astype(np.float32) + ((d >= 0) & (d % 4 == 0) & (d <= 512)).astype(np.float32) \
        + ((d >= 0) & (d % 16 == 0)).astype(np.float32)
    c[:, C_CM:C_CM + 1152] = cm
    c[:, C_ON:C_ON + 128] = 1.0
    return c


def _slab(w, ncols):
    L, K, N = w.shape
    return np.ascontiguousarray(w.reshape(L, K // 128, 128, N // ncols, ncols).transpose(0, 3, 2, 1, 4))


def _prep(inputs):
    f = lambda k: np.asarray(inputs[k], dtype=np.float32)
    L = DEPTH
    shared = {}
    shared["w_in"] = _slab(f("w_in"), 256)
    shared["w_out"] = _slab(f("w_mix_out"), 256)
    shared["w_q"] = _slab(f("w_mem_q"), 256)
    shared["w_kv"] = _slab(f("w_mem_kv"), 256)
    shared["w_o"] = _slab(f("w_mem_o"), 256)
    wu = f("w_up")
    shared["w_up"] = np.ascontiguousarray(wu.reshape(L, 16, 128, 2, NFC, 128).transpose(0, 4, 2, 3, 1, 5))
    wd = f("w_down")
    shared["w_dn"] = np.ascontiguousarray(wd.reshape(L, 2, 22, 128, 16, 128).transpose(0, 4, 1, 3, 2, 5))
    pp = np.zeros((128, L, NP), np.float32)
    for l in range(L):
        for i, k in enumerate(["ln1_g", "ln1_b", "ln2_g", "ln2_b", "ln3_g", "ln3_b"]):
            pp[:, l, PP_LN + 16 * i:PP_LN + 16 * (i + 1)] = f(k)[l].reshape(16, 128).T
        cw = f("conv_w")[l]
        for k in range(3):
            pp[:, l, PP_CW + 88 * k:PP_CW + 88 * (k + 1)] = cw[k].reshape(88, 128).T
        pp[:, l, PP_CB:PP_CB + 88] = f("conv_b")[l].reshape(88, 128).T
        pp[:, l, PP_GD] = f("g_diff")[l]
        pp[:, l, PP_GL] = f("g_dil")[l]
        for i, k in enumerate(["lambda_q1", "lambda_k1", "lambda_q2", "lambda_k2"]):
            pp[:, l, PP_LAM + 64 * i:PP_LAM + 64 * (i + 1)] = f(k)[l][None, :]
    shared["pp"] = pp
    shared["cst"] = _consts()
    x = f("x")
    mem = f("mem")
    in_maps = []
    for c in range(NCORES):
        m = dict(shared)
        m["x"] = np.ascontiguousarray(x[c * SPC:(c + 1) * SPC])
        m["mem"] = np.ascontiguousarray(mem[c * SPC:(c + 1) * SPC])
        in_maps.append(m)
    return in_maps


_CACHE = {}


def kernel(**inputs):
    in_maps = _prep(inputs)
    if "nc" not in _CACHE:
        _CACHE["nc"] = build_program()[0]
    nc = _CACHE["nc"]
    res = run_bass_kernel_spmd(nc, in_maps, core_ids=list(range(NCORES)))
    _CACHE["last"] = res.results
    out = np.concatenate([np.asarray(r["out"], dtype=np.float32) for r in res.results], axis=0)
    return out
```
